# Optimizing a Trainium2 kernel written in Bass

```python
import math
import jax, jax.numpy as jnp
from jax import lax
import numpy as np

D_MODEL = 1024
BATCH = 4
SEQ = 4096
DEPTH = 4

N_MIXERS = 2
N_DIFF = (DEPTH + 1) // 2
N_GLA = DEPTH // 2

DIFF_HEADS = 8
DIFF_HEAD_DIM = 64
DIFF_V_DIM = 2 * DIFF_HEAD_DIM
DIFF_WIDTH = DIFF_HEADS * 2 * DIFF_HEAD_DIM
ROPE_THETA = 10000.0
Q_BLOCK = 128
LAMBDA_INIT_STD = 0.1

GLA_HEADS = 4
GLA_KEY_DIM = D_MODEL // 2
GLA_VAL_DIM = D_MODEL
GLA_DK = GLA_KEY_DIM // GLA_HEADS
GLA_DV = GLA_VAL_DIM // GLA_HEADS
GLA_GATE_RANK = 16
GLA_GATE_TEMP = 16.0
GLA_CHUNK = 64

D_FF = 2816
CONV_WIDTH = 3

NORM_EPS = 1e-6

kernel_name = 'hybrid_diffattn_gla_convffn'


def rms_norm(x, gain):
    xf = x.astype(jnp.float32)
    y = xf * lax.rsqrt(jnp.mean(xf * xf, axis=-1, keepdims=True) + NORM_EPS)
    return (y * gain.astype(jnp.float32)).astype(x.dtype)


def rope_tables(seq, dim):
    inv = 1.0 / (ROPE_THETA ** (jnp.arange(0, dim, 2, dtype=jnp.float32) / dim))
    ang = jnp.arange(seq, dtype=jnp.float32)[:, None] * inv[None, :]
    return jnp.cos(ang), jnp.sin(ang)


def apply_rope(t, cos, sin):
    t1, t2 = jnp.split(t, 2, axis=-1)
    c = cos[None, :, None, None, :]
    s = sin[None, :, None, None, :]
    return jnp.concatenate([t1 * c - t2 * s, t2 * c + t1 * s], axis=-1)


def diff_attention(h, w_qkv, w_o, lam_vecs, subln_gain, lambda_init, cos, sin):
    B, S, _ = h.shape
    f32 = jnp.float32
    q, k, v = jnp.split(h @ w_qkv, 3, axis=-1)
    q = q.reshape(B, S, DIFF_HEADS, 2, DIFF_HEAD_DIM).astype(f32)
    k = k.reshape(B, S, DIFF_HEADS, 2, DIFF_HEAD_DIM).astype(f32)
    v = v.reshape(B, S, DIFF_HEADS, DIFF_V_DIM).astype(f32)
    q = apply_rope(q, cos, sin) * (DIFF_HEAD_DIM ** -0.5)
    k = apply_rope(k, cos, sin)
    lv = lam_vecs.astype(f32)
    lam = jnp.exp(jnp.sum(lv[0] * lv[1])) - jnp.exp(jnp.sum(lv[2] * lv[3])) + lambda_init
    nb = S // Q_BLOCK
    q_blocks = q.reshape(B, nb, Q_BLOCK, DIFF_HEADS, 2, DIFF_HEAD_DIM).transpose(1, 0, 2, 3, 4, 5)
    k_pos = jnp.arange(S)

    def block(args):
        qb, start = args
        s = jnp.einsum('bqhmd,bkhmd->bhmqk', qb, k)
        q_pos = start + jnp.arange(Q_BLOCK)
        mask = k_pos[None, :] <= q_pos[:, None]
        p = jax.nn.softmax(jnp.where(mask, s, -jnp.inf), axis=-1)
        a = p[:, :, 0] - lam * p[:, :, 1]
        return jnp.einsum('bhqk,bkhe->bqhe', a, v)

    starts = jnp.arange(nb) * Q_BLOCK
    o = lax.map(block, (q_blocks, starts))
    o = o.transpose(1, 0, 2, 3, 4).reshape(B, S, DIFF_HEADS, DIFF_V_DIM)
    o = rms_norm(o, subln_gain) * (1.0 - lambda_init)
    return o.reshape(B, S, DIFF_WIDTH).astype(h.dtype) @ w_o


def gla(h, w_in, w_a1, w_a2, b_a, norm_gain, w_o):
    B, S, _ = h.shape
    f32 = jnp.float32
    C = GLA_CHUNK
    nc = S // C
    q, k, v, r = jnp.split(h @ w_in, [GLA_KEY_DIM, 2 * GLA_KEY_DIM, 2 * GLA_KEY_DIM + GLA_VAL_DIM], axis=-1)
    g = jax.nn.log_sigmoid(((h @ w_a1) @ w_a2 + b_a).astype(f32)) / GLA_GATE_TEMP

    def to_chunks(t, dh):
        return t.astype(f32).reshape(B, nc, C, GLA_HEADS, dh).transpose(1, 0, 3, 2, 4)

    qc = to_chunks(q, GLA_DK) * (GLA_DK ** -0.5)
    kc = to_chunks(k, GLA_DK)
    vc = to_chunks(v, GLA_DV)
    gc = to_chunks(g, GLA_DK)
    causal = jnp.tril(jnp.ones((C, C), dtype=bool))[None, None, :, :, None]

    def step(state, inp):
        qi, ki, vi, gi = inp
        b = jnp.cumsum(gi, axis=2)
        o_inter = jnp.einsum('bhcd,bhde->bhce', qi * jnp.exp(b), state)
        rel = b[:, :, :, None, :] - b[:, :, None, :, :]
        decay = jnp.exp(jnp.where(causal, rel, -jnp.inf))
        attn = jnp.einsum('bhid,bhjd,bhijd->bhij', qi, ki, decay)
        o = o_inter + jnp.einsum('bhij,bhje->bhie', attn, vi)
        b_last = b[:, :, -1:, :]
        state = jnp.exp(b_last[:, :, 0, :])[..., None] * state + jnp.einsum(
            'bhcd,bhce->bhde', ki * jnp.exp(b_last - b), vi)
        return state, o

    s0 = jnp.zeros((B, GLA_HEADS, GLA_DK, GLA_DV), f32)
    _, o = lax.scan(step, s0, (qc, kc, vc, gc))
    o = o.transpose(1, 0, 3, 2, 4).reshape(B, S, GLA_HEADS, GLA_DV)
    o = rms_norm(o, norm_gain) * jax.nn.silu(r.astype(f32).reshape(B, S, GLA_HEADS, GLA_DV))
    return o.reshape(B, S, GLA_VAL_DIM).astype(h.dtype) @ w_o


def conv_ffn(h, w_up, conv_w, conv_b, w_down):
    u = h @ w_up
    u = lax.conv_general_dilated(
        u, conv_w[:, None, :], window_strides=(1,), padding=((CONV_WIDTH - 1, 0),),
        dimension_numbers=('NWC', 'WIO', 'NWC'), feature_group_count=2 * D_FF) + conv_b
    gate, up = jnp.split(u, 2, axis=-1)
    return (jax.nn.silu(gate) * up) @ w_down


def setup_inputs(seed: int = 0) -> dict:
    key = jax.random.key(seed)
    ks = jax.random.split(key, 20)
    f32 = jnp.float32

    def w(k, shape, fan_in):
        return jax.random.normal(k, shape, f32) * (fan_in ** -0.5)

    def gain(k, shape):
        return 1.0 + 0.01 * jax.random.normal(k, shape, f32)

    return {
        'x': jax.random.normal(ks[0], (BATCH, SEQ, D_MODEL), f32),
        'norm_mix': gain(ks[1], (DEPTH, D_MODEL)),
        'norm_ffn': gain(ks[2], (DEPTH, D_MODEL)),
        'norm_final': gain(ks[3], (D_MODEL,)),
        'diff_w_qkv': w(ks[4], (N_DIFF, D_MODEL, 3 * DIFF_WIDTH), D_MODEL),
        'diff_w_o': w(ks[5], (N_DIFF, DIFF_WIDTH, D_MODEL), DIFF_WIDTH),
        'diff_lambda': LAMBDA_INIT_STD * jax.random.normal(ks[6], (N_DIFF, 4, DIFF_HEAD_DIM), f32),
        'diff_subln': gain(ks[7], (N_DIFF, DIFF_V_DIM)),
        'gla_w_in': w(ks[8], (N_GLA, D_MODEL, 2 * GLA_KEY_DIM + 2 * GLA_VAL_DIM), D_MODEL),
        'gla_w_a1': w(ks[9], (N_GLA, D_MODEL, GLA_GATE_RANK), D_MODEL),
        'gla_w_a2': w(ks[10], (N_GLA, GLA_GATE_RANK, GLA_KEY_DIM), GLA_GATE_RANK),
        'gla_b_a': 0.1 * jax.random.normal(ks[11], (N_GLA, GLA_KEY_DIM), f32),
        'gla_norm': gain(ks[12], (N_GLA, GLA_DV)),
        'gla_w_o': w(ks[13], (N_GLA, GLA_VAL_DIM, D_MODEL), GLA_VAL_DIM),
        'ffn_w_up': w(ks[14], (DEPTH, D_MODEL, 2 * D_FF), D_MODEL),
        'ffn_conv_w': w(ks[15], (DEPTH, CONV_WIDTH, 2 * D_FF), CONV_WIDTH),
        'ffn_conv_b': 0.01 * jax.random.normal(ks[16], (DEPTH, 2 * D_FF), f32),
        'ffn_w_down': w(ks[17], (DEPTH, D_FF, D_MODEL), D_FF),
    }


def reference(x, norm_mix, norm_ffn, norm_final, diff_w_qkv, diff_w_o, diff_lambda, diff_subln,
              gla_w_in, gla_w_a1, gla_w_a2, gla_b_a, gla_norm, gla_w_o,
              ffn_w_up, ffn_conv_w, ffn_conv_b, ffn_w_down):
    cos, sin = rope_tables(x.shape[1], DIFF_HEAD_DIM)
    h = x
    for layer in range(DEPTH):
        j = layer // N_MIXERS
        hn = rms_norm(h, norm_mix[layer])
        if layer % N_MIXERS == 0:
            lambda_init = 0.8 - 0.6 * math.exp(-0.3 * layer)
            h = h + diff_attention(hn, diff_w_qkv[j], diff_w_o[j], diff_lambda[j], diff_subln[j],
                                   lambda_init, cos, sin)
        else:
            h = h + gla(hn, gla_w_in[j], gla_w_a1[j], gla_w_a2[j], gla_b_a[j], gla_norm[j], gla_w_o[j])
        h = h + conv_ffn(rms_norm(h, norm_ffn[layer]), ffn_w_up[layer], ffn_conv_w[layer],
                         ffn_conv_b[layer], ffn_w_down[layer])
    return rms_norm(h, norm_final)
```

```python
import numpy as np
import ml_dtypes
import concourse.bass as bass
import concourse.mybir as mybir
from concourse.bass_utils import run_bass_kernel_spmd

F32 = mybir.dt.float32
BF16 = mybir.dt.bfloat16
I32 = mybir.dt.int32
AF = mybir.ActivationFunctionType
ALU = mybir.AluOpType
AX = mybir.AxisListType

D = 1024
KC = 8
S = 4096
NT = 2048
TT = 512
NTILE = NT // TT
DFF = 2816
NJ = DFF // 128
DEPTH = 4
EPS = 1e-6
SB_LO = 16640
DBG = {}
SB_HI = 229000


class Buf:
    __slots__ = ("name", "last_w", "readers", "excl")

    def __init__(self, name, excl=False):
        self.name = name
        self.last_w = None
        self.readers = {}
        self.excl = excl


class Op:
    __slots__ = ("eng", "fn", "deps", "signal", "ev", "dma_key", "idx", "cc")

    def __init__(self, eng, fn, dma_key=None, cc=False):
        self.eng = eng
        self.fn = fn
        self.deps = set()
        self.signal = False
        self.ev = None
        self.dma_key = dma_key
        self.cc = cc


class Prog:
    ENGS = ("pe", "act", "dve", "pool", "sp")

    def __init__(self, nc):
        self.nc = nc
        self.ops = []
        self.last_on = {}
        self.bar = {}
        self.dma_since_bar = {}

    def op(self, eng, fn, reads=(), writes=(), dma_key=None, cc=False):
        o = Op(eng, fn, dma_key, cc)
        o.idx = len(self.ops)
        deps = o.deps
        for b in reads:
            if b.last_w is not None:
                deps.add(b.last_w)
            if b.excl:
                for r in b.readers.values():
                    if r.eng != eng:
                        deps.add(r)
        for b in writes:
            if b.last_w is not None:
                deps.add(b.last_w)
            for r in b.readers.values():
                deps.add(r)
        if eng in self.bar:
            deps.update(self.bar.pop(eng))
        deps.discard(o)
        isdma = dma_key is not None or cc
        rk = ("dma", dma_key) if isdma else eng
        for b in writes:
            b.last_w = o
            b.readers = {}
        for b in reads:
            b.readers[rk] = o
        for d in deps:
            d.signal = True
        if isdma:
            o.signal = True
            self.dma_since_bar[("cc", o.idx) if cc else dma_key] = o
        else:
            self.last_on[eng] = o
        self.ops.append(o)
        return o

    def barrier(self):
        tails = set(self.last_on.values()) | set(self.dma_since_bar.values())
        for t in tails:
            t.signal = True
        self.dma_since_bar = {}
        for e in self.ENGS:
            s = self.bar.get(e, set())
            s |= tails
            self.bar[e] = s

    def emit(self):
        nc = self.nc
        sem_eng = {e: nc.alloc_semaphore("s_" + e) for e in ("pe", "act", "dve", "pool")}
        dma_sems = {}
        cnt = {e: 0 for e in sem_eng}
        cum = {}
        cc_sem = nc.alloc_semaphore("s_cc")
        ncc = 0
        for o in self.ops:
            if o.cc:
                ncc += 1
                o.ev = (cc_sem, ncc)
            elif o.dma_key is not None:
                if o.dma_key not in dma_sems:
                    dma_sems[o.dma_key] = nc.alloc_semaphore("d_%d" % len(dma_sems))
                    cum[o.dma_key] = 0
                cum[o.dma_key] += 16
                o.ev = (dma_sems[o.dma_key], cum[o.dma_key])
            elif o.signal:
                cnt[o.eng] += 1
                o.ev = (sem_eng[o.eng], cnt[o.eng])
        self.n_sems = len(dma_sems) + 5
        engobj = {"pe": "tensor", "act": "scalar", "dve": "vector", "pool": "gpsimd", "sp": "sync"}
        by_eng = {e: [o for o in self.ops if o.eng == e] for e in self.ENGS}

        def run(e, eng):
            waited = {}
            for o in by_eng[e]:
                for d in sorted(o.deps, key=lambda d: d.idx):
                    if d.eng == "pe" and e == "pe" and d.dma_key is None:
                        continue
                    sem, val = d.ev
                    k = id(sem)
                    if waited.get(k, 0) < val:
                        eng.wait_ge(sem, val)
                        waited[k] = val
                ins = o.fn(eng)
                if o.cc:
                    ins.then_inc(o.ev[0])
                elif o.dma_key is not None:
                    ins.then_inc(o.ev[0], 16)
                elif o.signal:
                    ins.then_inc(o.ev[0], 1)
            for key, sem in dma_sems.items():
                pass

        with nc.Block() as block:
            for e in self.ENGS:
                if not by_eng[e]:
                    continue
                getattr(block, engobj[e])(lambda eng, e=e: run(e, eng))


class Arena:
    def __init__(self, nc):
        self.nc = nc
        self.ptr = SB_LO
        self.n = 0
        self.marks = []

    def alloc(self, shape, dtype, name="t"):
        esz = 4 if dtype in (F32, I32) else 2
        nbytes = int(np.prod(shape[1:])) * esz
        self.ptr = (self.ptr + 63) // 64 * 64
        off = self.ptr
        self.ptr += nbytes
        assert self.ptr <= SB_HI, "SBUF arena overflow: %d" % self.ptr
        self.n += 1
        t = self.nc.alloc_sbuf_tensor_at("%s_%d" % (name, self.n), list(shape), dtype, offset=off)
        return t

    def mark(self):
        self.marks.append(self.ptr)

    def release(self):
        self.ptr = self.marks.pop()


class T:
    def __init__(self, t, name="t"):
        self.t = t
        self.b = Buf(name)

    def __getitem__(self, idx):
        return self.t[idx]


class Ring:
    def __init__(self, items):
        self.items = items
        self.i = 0

    def next(self):
        it = self.items[self.i % len(self.items)]
        self.i += 1
        return it


class Builder:
    def __init__(self, nc):
        self.nc = nc
        self.P = Prog(nc)
        self.A = Arena(nc)
        self.psum = nc.alloc_psum_tensor("ps", [128, 4096], F32)
        self.banks = [T(self.psum[:, i * 512:(i + 1) * 512], "bank%d" % i) for i in range(8)]
        for bk in self.banks:
            bk.b.excl = True
        self.bank_ring = Ring(self.banks[:7])
        self.dram = {}
        self.ones = self.sb([128, 128], BF16, "ones")
        self.P.op("dve", lambda e: e.memset(self.ones[:, :], 1.0), writes=[self.ones.b])
        self.epsc = self.sb([128, 1], F32, "epsc")
        self.P.op("dve", lambda e: e.memset(self.epsc[:, :], EPS), writes=[self.epsc.b])

    def sb(self, shape, dtype, name="t"):
        return T(self.A.alloc(shape, dtype, name), name)

    def din(self, name, shape, dtype):
        t = T(self.nc.dram_tensor(name, list(shape), dtype, kind="ExternalInput"), name)
        self.dram[name] = t
        return t

    def dout(self, name, shape, dtype):
        t = T(self.nc.dram_tensor(name, list(shape), dtype, kind="ExternalOutput"), name)
        self.dram[name] = t
        return t

    def dscratch(self, name, shape, dtype):
        t = T(self.nc.dram_tensor(name, list(shape), dtype), name)
        self.dram[name] = t
        return t

    def dma(self, q, out_ap, in_ap, reads, writes, key):
        return self.P.op(q, lambda e: e.dma_start(out=out_ap, in_=in_ap), reads=reads, writes=writes, dma_key=key)

    def mm_group(self, bank_ap, pairs, reads, writes):
        n = len(pairs)

        def fn(e):
            ins = None
            for i, (l, r) in enumerate(pairs):
                ins = e.matmul(bank_ap, l, r, start=(i == 0), stop=(i == n - 1))
            return ins

        return self.P.op("pe", fn, reads=reads, writes=writes)

    def rms_stats(self, src, ntok, sq, bank, rstd):
        P = self.P
        P.op("act", lambda e: e.activation(sq[:, :, 0:ntok], src[:, :, 0:ntok], AF.Square),
             reads=[src.b], writes=[sq.b])
        self.mm_group(bank[:, 0:ntok], [(self.ones[:, :], sq[:, kc, 0:ntok]) for kc in range(KC)],
                      reads=[sq.b, self.ones.b], writes=[bank.b])
        P.op("act", lambda e: e.activation(rstd[:, 0:ntok], bank[:, 0:ntok], AF.Sqrt, bias=self.epsc[:, 0:1],
                                           scale=1.0 / D),
             reads=[bank.b, self.epsc.b], writes=[rstd.b])
        P.op("dve", lambda e: e.reciprocal(rstd[:, 0:ntok], rstd[:, 0:ntok]),
             reads=[rstd.b], writes=[rstd.b])

    def phase_B(self, h_in, h, oin, hhalo, ohalo, wo_d, wup_d, wdn_d, sp_d, hscale_d, out_hn, out_final):
        P, A = self.P, self.A
        A.mark()
        spt = self.sb([128, NSP], F32, "sp")
        hsc = self.sb([128, 1], F32, "hsc")
        g32a = self.sb([128, KC], F32, "g32a")
        g32b = self.sb([128, KC], F32, "g32b")
        hn = self.sb([128, KC, NT], BF16, "hn")
        hnh = self.sb([128, KC, 2], BF16, "hnh")
        act = self.sb([128, NJ, NT], BF16, "act")
        self.dma("sp", spt[:, :], sp_d[:, :], [sp_d.b], [spt.b], "sp_small")
        self.dma("sp", hsc[:, :], hscale_d[:, :], [hscale_d.b], [hsc.b], "sp_small2")
        P.op("dve", lambda e: e.tensor_scalar(g32a[:, :], spt[:, SP_GFFN:SP_GFFN + KC], 1.0, None, ALU.mult),
             reads=[spt.b], writes=[g32a.b])
        P.op("dve", lambda e: e.tensor_scalar(g32b[:, :], spt[:, SP_GNEXT:SP_GNEXT + KC], 1.0, None, ALU.mult),
             reads=[spt.b], writes=[g32b.b])

        A.mark()
        wo = self.sb([128, KC, D], BF16, "wo")
        self.dma("pool", wo[:, :, :], wo_d[:, :].rearrange("p (a c) -> p a c", c=D), [wo_d.b], [wo.b], "wo")
        oin_s = [self.sb([128, KC, TT], BF16, "oin") for _ in range(2)]
        ht_s = [self.sb([128, KC, TT], F32, "ht") for _ in range(2)]
        sq = self.sb([128, KC, TT], BF16, "sq")
        rstd = self.sb([128, TT], F32, "rstd")
        tiles = [(2, hhalo[:, :, :], ohalo[:, :, :], None)]
        for ti in range(NTILE):
            sl = slice(ti * TT, (ti + 1) * TT)
            tiles.append((TT, h_in[:, :, sl], oin[:, :, sl], sl))

        def b1_load(i):
            ntok, h_src, o_src, sl = tiles[i]
            k = i % 2
            self.dma("sp", oin_s[k][:, :, 0:ntok], o_src, [oin.b, ohalo.b], [oin_s[k].b], "oin%d" % k)
            self.dma("sp", ht_s[k][:, :, 0:ntok], h_src, [h_in.b, hhalo.b], [ht_s[k].b], "ht%d" % k)

        b1_load(0)
        for i in range(len(tiles)):
            if i + 1 < len(tiles):
                b1_load(i + 1)
            ntok, _, _, sl = tiles[i]
            ot, ht = oin_s[i % 2], ht_s[i % 2]
            for fo in range(KC):
                bk = self.bank_ring.next()
                self.mm_group(bk[:, 0:ntok],
                              [(wo[:, kc, fo * 128:(fo + 1) * 128], ot[:, kc, 0:ntok]) for kc in range(KC)],
                              reads=[wo.b, ot.b], writes=[bk.b])
                P.op("dve", lambda e, bk=bk, fo=fo, ht=ht, ntok=ntok: e.tensor_tensor(
                    ht[:, fo, 0:ntok], ht[:, fo, 0:ntok], bk[:, 0:ntok], ALU.add),
                    reads=[bk.b, ht.b], writes=[ht.b])
            if sl is not None:
                self.dma("sp", h[:, :, sl], ht[:, :, 0:ntok], [ht.b], [h.b], "hst%d" % (i % 2))
            bk = self.bank_ring.next()
            self.rms_stats(ht, ntok, sq, bk, rstd)
            for kc in range(KC):
                dst = hnh[:, kc, :] if sl is None else hn[:, kc, sl]
                P.op("dve", lambda e, kc=kc, ht=ht, ntok=ntok, dst=dst: e.scalar_tensor_tensor(
                    dst, ht[:, kc, 0:ntok], g32a[:, kc:kc + 1], rstd[:, 0:ntok], ALU.mult, ALU.mult),
                    reads=[ht.b, g32a.b, rstd.b], writes=[hnh.b if sl is None else hn.b])
        P.barrier()
        A.release()

        A.mark()
        wup_s = [self.sb([128, KC, 256], BF16, "wup") for _ in range(3)]
        ub_s = [(self.sb([128, NT + 2], F32, "ubg"), self.sb([128, NT + 2], F32, "ubu")) for _ in range(2)]
        xg = self.sb([128, NT], F32, "xg")
        xu = self.sb([128, NT], F32, "xu")
        hb = self.banks[7]
        hslots = [hb.b for i in range(8)]

        def load_wup(j):
            w = wup_s[j % 3]
            self.dma("pool", w[:, :, :], wup_d[j].rearrange("p (a c) -> p a c", c=256), [wup_d.b], [w.b],
                     "wup%d" % (j % 3))

        load_wup(0)
        load_wup(1)
        for j in range(NJ):
            w = wup_s[j % 3]
            if j + 2 < NJ:
                load_wup(j + 2)
            ubg, ubu = ub_s[j % 2]
            cg = SP_CONV + j * 4
            cu = SP_CONV + (NJ + j) * 4
            hs = hslots[j % 8]
            hc = (j % 8) * 8
            for half, ub in ((0, ubg), (1, ubu)):
                self.mm_group(hb[:, hc + half * 2:hc + half * 2 + 2],
                              [(w[:, kc, half * 128:(half + 1) * 128], hnh[:, kc, :]) for kc in range(KC)],
                              reads=[w.b, hnh.b], writes=[hs])
                P.op("act", lambda e, ub=ub, half=half, hc=hc: e.activation(
                    ub[:, 0:2], hb[:, hc + half * 2:hc + half * 2 + 2], AF.Copy, scale=hsc[:, 0:1]),
                    reads=[hs, hsc.b], writes=[ub.b])
            for ti in range(NTILE):
                for half, ub in ((0, ubg), (1, ubu)):
                    bk = self.bank_ring.next()
                    self.mm_group(bk[:, :], [(w[:, kc, half * 128:(half + 1) * 128], hn[:, kc, ti * TT:(ti + 1) * TT])
                                             for kc in range(KC)], reads=[w.b, hn.b], writes=[bk.b])
                    P.op("act", lambda e, ub=ub, bk=bk, ti=ti: e.activation(
                        ub[:, 2 + ti * TT:2 + (ti + 1) * TT], bk[:, :], AF.Copy),
                        reads=[bk.b], writes=[ub.b])
            for ub, x, c in ((ubg, xg, cg), (ubu, xu, cu)):
                P.op("act", lambda e, ub=ub, x=x, c=c: e.activation(
                    x[:, :], ub[:, 2:NT + 2], AF.Identity, bias=spt[:, c + 3:c + 4], scale=spt[:, c + 2:c + 3]),
                    reads=[ub.b, spt.b], writes=[x.b])
                P.op("dve", lambda e, ub=ub, x=x, c=c: e.scalar_tensor_tensor(
                    x[:, :], ub[:, 1:NT + 1], spt[:, c + 1:c + 2], x[:, :], ALU.mult, ALU.add),
                    reads=[ub.b, spt.b, x.b], writes=[x.b])
                P.op("dve", lambda e, ub=ub, x=x, c=c: e.scalar_tensor_tensor(
                    x[:, :], ub[:, 0:NT], spt[:, c:c + 1], x[:, :], ALU.mult, ALU.add),
                    reads=[ub.b, spt.b, x.b], writes=[x.b])
            P.op("act", lambda e, ubg=ubg: e.activation(ubg[:, 0:NT], xg[:, :], AF.Silu),
                 reads=[xg.b], writes=[ubg.b])
            P.op("dve", lambda e, ubg=ubg, j=j: e.tensor_tensor(act[:, j, :], ubg[:, 0:NT], xu[:, :], ALU.mult),
                 reads=[ubg.b, xu.b], writes=[act.b])
        P.barrier()
        A.release()

        A.mark()
        wdn_s = [self.sb([128, NJ * 128], BF16, "wdn") for _ in range(2)]
        hrow_s = [self.sb([128, NT], F32, "hrow") for _ in range(2)]

        def load_b3(fo):
            w = wdn_s[fo % 2]
            self.dma("pool", w[:, :].rearrange("p (a c) -> p a c", a=2),
                     wdn_d[fo].rearrange("p (a c) -> p a c", a=2), [wdn_d.b], [w.b], "wdn%d" % (fo % 2))
            self.dma("sp", hrow_s[fo % 2][:, :], h[:, fo, :], [h.b], [hrow_s[fo % 2].b], "hrow%d" % (fo % 2))

        load_b3(0)
        for fo in range(KC):
            w = wdn_s[fo % 2]
            hr = hrow_s[fo % 2]
            if fo + 1 < KC:
                load_b3(fo + 1)
            for ti in range(NTILE):
                bk = self.bank_ring.next()
                self.mm_group(bk[:, :], [(w[:, j * 128:(j + 1) * 128], act[:, j, ti * TT:(ti + 1) * TT])
                                         for j in range(NJ)], reads=[w.b, act.b], writes=[bk.b])
                P.op("dve", lambda e, hr=hr, bk=bk, ti=ti: e.tensor_tensor(
                    hr[:, ti * TT:(ti + 1) * TT], hr[:, ti * TT:(ti + 1) * TT], bk[:, :], ALU.add),
                    reads=[bk.b, hr.b], writes=[hr.b])
            self.dma("sp", h[:, fo, :], hr[:, :], [hr.b], [h.b], "hrow%d" % (fo % 2))
        P.barrier()
        A.release()

        A.mark()
        ht4_s = [self.sb([128, KC, TT], F32, "ht4") for _ in range(2)]
        sq4 = self.sb([128, KC, TT], BF16, "sq4")
        rstd4 = self.sb([128, TT], F32, "rstd4")
        if out_final is not None:
            o_s = [self.sb([128, KC, TT], F32, "of") for _ in range(2)]
            od = out_final
        else:
            o_s = [self.sb([128, KC, TT], BF16, "ohn") for _ in range(2)]
            od = out_hn
        for ti in range(NTILE):
            sl = slice(ti * TT, (ti + 1) * TT)
            k = ti % 2
            ht = ht4_s[k]
            self.dma("sp", ht[:, :, :], h[:, :, sl], [h.b], [ht.b], "ht4%d" % k)
            bk = self.bank_ring.next()
            self.rms_stats(ht, TT, sq4, bk, rstd4)
            ot = o_s[k]
            for kc in range(KC):
                P.op("dve", lambda e, kc=kc, ht=ht, ot=ot: e.scalar_tensor_tensor(
                    ot[:, kc, :], ht[:, kc, :], g32b[:, kc:kc + 1], rstd4[:, :], ALU.mult, ALU.mult),
                    reads=[ht.b, g32b.b, rstd4.b], writes=[ot.b])
            self.dma("sp", od[:, :, sl], ot[:, :, :], [ot.b], [od.b], "o4%d" % k)
        P.barrier()
        A.release()
        A.release()

    def phase_A_diff(self, hn_t, wq_d, wk_d, wv_d, ropeC_d, ropeS_d, cst_d, spa_d, o_d):
        P, A = self.P, self.A
        NH = 4
        A.mark()
        QT = self.sb([128, NH, S], BF16, "QT")
        KT = self.sb([128, NH, S], BF16, "KT")
        V = self.sb([128, S // 128, NH * 128], BF16, "V")
        cst = self.sb([128, 128 + 1024], BF16, "cst")
        spa = self.sb([128, 259], F32, "spa")
        self.dma("sp", cst[:, :], cst_d[:, :], [cst_d.b], [cst.b], "cstA")
        self.dma("sp", spa[:, :], spa_d[:, :], [spa_d.b], [spa.b], "spaA")
        pr = self.sb([128, 2, 64], F32, "lampr")
        ls = self.sb([128, 4], F32, "lams")
        nlam = self.sb([128, 1], F32, "nlam")
        gsub = self.sb([128, 1], F32, "gsub")
        if not DBG.get("nolam"):
          P.op("dve", lambda e: e.tensor_tensor(pr[:, :, :], spa[:, 0:128].rearrange("p (v d) -> p v d", d=64),
                                                spa[:, 128:256].rearrange("p (v d) -> p v d", d=64), ALU.mult),
               reads=[spa.b], writes=[pr.b])
          P.op("dve", lambda e: e.reduce_sum(ls[:, 0:2], pr[:, :, :], axis=AX.X), reads=[pr.b], writes=[ls.b])
          P.op("act", lambda e: e.activation(ls[:, 2:4], ls[:, 0:2], AF.Exp), reads=[ls.b], writes=[ls.b])
          P.op("dve", lambda e: e.tensor_tensor(nlam[:, :], ls[:, 3:4], ls[:, 2:3], ALU.subtract),
               reads=[ls.b], writes=[nlam.b])
          P.op("dve", lambda e: e.tensor_scalar(nlam[:, :], nlam[:, :], spa[:, 257:258], None, ALU.add),
               reads=[nlam.b, spa.b], writes=[nlam.b])
          P.op("dve", lambda e: e.tensor_scalar(gsub[:, :], spa[:, 256:257], spa[:, 258:259], None, ALU.mult),
               reads=[spa.b], writes=[gsub.b])

        A.mark()
        wq = self.sb([128, NH, KC, 128], BF16, "wq")
        wk = self.sb([128, NH, KC, 128], BF16, "wk")
        wv = self.sb([128, KC, NH * 128], BF16, "wv")
        for hl in range(NH):
            self.dma("pool", wq[:, hl, :, :], wq_d[hl].rearrange("p (a c) -> p a c", c=128), [wq_d.b], [wq.b], "wq")
            self.dma("pool", wk[:, hl, :, :], wk_d[hl].rearrange("p (a c) -> p a c", c=128), [wk_d.b], [wk.b], "wk")
        self.dma("pool", wv[:, :, :], wv_d[:, :].rearrange("p (a c) -> p a c", c=NH * 128), [wv_d.b], [wv.b], "wv")
        hnt_s = [self.sb([128, KC, TT], BF16, "hnt") for _ in range(2)]
        rc_s = [self.sb([128, TT], F32, "rc") for _ in range(2)]
        rs_s = [self.sb([128, TT], F32, "rs") for _ in range(2)]
        tb_s = [self.sb([128, TT], BF16, "tb") for _ in range(2)]
        ra_s = [self.sb([128, TT], F32, "ra") for _ in range(2)]
        rb_s = [self.sb([128, TT], F32, "rb") for _ in range(2)]
        NTL = S // TT

        def a1_load(i):
            k = i % 2
            sl = slice(i * TT, (i + 1) * TT)
            self.dma("sp", hnt_s[k][:, :, :], hn_t[0](i), [hn_t[1]], [hnt_s[k].b], "hnt%d" % k)
            self.dma("sp", rc_s[k][:, :], ropeC_d[:, sl], [ropeC_d.b], [rc_s[k].b], "rc%d" % k)
            self.dma("sp", rs_s[k][:, :], ropeS_d[:, sl], [ropeS_d.b], [rs_s[k].b], "rs%d" % k)

        a1_load(0)
        cnt = 0
        for i in range(DBG.get('maxtile', NTL)):
            if i + 1 < NTL:
                a1_load(i + 1)
            k = i % 2
            hnt, rc, rs = hnt_s[k], rc_s[k], rs_s[k]
            sl = slice(i * TT, (i + 1) * TT)
            for which, wt, dstT in ((0, wq, QT), (1, wk, KT)):
                for hl in range(NH):
                    tb, ra, rb = tb_s[cnt % 2], ra_s[cnt % 2], rb_s[cnt % 2]
                    cnt += 1
                    b1 = self.bank_ring.next()
                    self.mm_group(b1[:, :], [(wt[:, hl, kc, :], hnt[:, kc, :]) for kc in range(KC)],
                                  reads=[wt.b, hnt.b], writes=[b1.b])
                    P.op("act", lambda e, tb=tb, b1=b1: e.activation(tb[:, :], b1[:, :], AF.Copy),
                         reads=[b1.b], writes=[tb.b])
                    if DBG.get("noelt"):
                        continue
                    if not DBG.get("nomult"):
                      P.op("dve", lambda e, ra=ra, b1=b1, rc=rc: e.tensor_tensor(ra[:, :], rc[:, :], b1[:, :], ALU.mult),
                         reads=[b1.b, rc.b], writes=[ra.b])
                    if not DBG.get("noswap"):
                        b2 = self.bank_ring.next()
                        self.mm_group(b2[:, :], [(cst[:, 0:128], tb[:, :])], reads=[cst.b, tb.b], writes=[b2.b])
                        P.op("dve", lambda e, rb=rb, b2=b2, rs=rs: e.tensor_tensor(rb[:, :], rs[:, :], b2[:, :], ALU.mult),
                             reads=[b2.b, rs.b], writes=[rb.b])
                    else:
                        P.op("dve", lambda e, rb=rb, rs=rs: e.tensor_copy(rb[:, :], rs[:, :]), reads=[rs.b], writes=[rb.b])
                    if DBG.get("noadd"):
                        continue
                    P.op(DBG.get("addeng", "pool"), lambda e, ra=ra, rb=rb, dstT=dstT, hl=hl, sl=sl: e.tensor_tensor(
                        dstT[:, hl, sl], ra[:, :], rb[:, :], ALU.add),
                        reads=[ra.b, rb.b], writes=[dstT.b])
            for tc4 in range(0 if DBG.get('nov') else TT // 128):
                bv = self.bank_ring.next()
                self.mm_group(bv[:, :], [(hnt[:, kc, tc4 * 128:(tc4 + 1) * 128], wv[:, kc, :]) for kc in range(KC)],
                              reads=[wv.b, hnt.b], writes=[bv.b])
                P.op("act", lambda e, bv=bv, i=i, tc4=tc4: e.activation(V[:, i * 4 + tc4, :], bv[:, :], AF.Copy),
                     reads=[bv.b], writes=[V.b])
        P.barrier()
        A.release()
        if DBG.get("a1only"):
            src = {"q": QT, "k": KT}[DBG["a1only"]]
            for hl in range(NH):
                self.dma("sp", o_d[hl][:, :], src[:, hl, :], [src.b], [o_d.b], "dbg")
            P.barrier()
            A.release()
            return

        A.mark()
        QG = 256
        pt_s = [self.sb([128, 512], BF16, "pt") for _ in range(4)]
        rl = self.sb([128, 512], F32, "rl")
        aa = self.sb([128, 512], F32, "aa")
        dd = self.sb([128, QG], F32, "dd")
        dsq = self.sb([128, QG], BF16, "dsq")
        rr = self.sb([128, QG], F32, "rr")
        oo_s = [self.sb([128, QG], BF16, "oo") for _ in range(2)]
        sS = [Buf("sS0", True), Buf("sS1", True)]
        pSl = [self.psum[:, 0:1024].rearrange("p (b c) -> p b c", b=2),
               self.psum[:, 1024:2048].rearrange("p (b c) -> p b c", b=2)]
        bO = [self.banks[4], self.banks[5]]
        bL = [self.banks[6], self.banks[6]]
        bR = self.banks[7]
        steps = []
        for hl in range(NH):
            for g in range(S // QG):
                nk = 2 * g + 2
                for ki in range(nk):
                    steps.append((hl, g, ki, nk))

        def emit_sc(i):
            hl, g, ki, nk = steps[i]
            pS = pSl[i % 2]
            ks = slice(ki * 128, (ki + 1) * 128)
            qs = slice(g * QG, (g + 1) * QG)

            def sc(e):
                e.matmul(pS[:, 0, 0:QG], KT[0:64, hl, ks], QT[0:64, hl, qs], start=True, stop=True)
                return e.matmul(pS[:, 1, 0:QG], KT[64:128, hl, ks], QT[64:128, hl, qs], start=True, stop=True)

            P.op("pe", sc, reads=[KT.b, QT.b], writes=[sS[i % 2]])

        emit_sc(0)
        grp = 0
        for i, (hl, g, ki, nk) in enumerate(steps):
            if i + 1 < len(steps):
                emit_sc(i + 1)
            gi = hl * (S // QG) + g
            bo, bl = bO[gi % 2], bL[gi % 2]
            pt = pt_s[i % 4]
            pS = pSl[i % 2]
            P.op("act", lambda e, pt=pt, pS=pS: e.activation(pt[:, :].rearrange("p (b c) -> p b c", b=2),
                                                            pS[:, :, 0:QG], AF.Exp, scale=0.125),
                 reads=[sS[i % 2]], writes=[pt.b])
            if ki >= nk - 2:
                mo = 128 + (ki - (nk - 2)) * 512
                P.op("dve", lambda e, pt=pt, mo=mo: e.tensor_tensor(pt[:, :], pt[:, :], cst[:, mo:mo + 512], ALU.mult),
                     reads=[pt.b, cst.b], writes=[pt.b])

            def pv(e, bo=bo, bl=bl, pt=pt, ki=ki, nk=nk, hl=hl):
                e.matmul(bo[:, :], V[:, ki, hl * 128:(hl + 1) * 128], pt[:, :], start=(ki == 0), stop=(ki == nk - 1))
                return e.matmul(bl[:, :], self.ones[:, :], pt[:, :], start=(ki == 0), stop=(ki == nk - 1))

            P.op("pe", pv, reads=[V.b, pt.b, self.ones.b], writes=[bo.b, bl.b])
            if ki == nk - 1:
                qs = slice(g * QG, (g + 1) * QG)
                oo = oo_s[grp % 2]
                P.op("dve", lambda e, bl=bl: e.reciprocal(rl[:, :], bl[:, :]), reads=[bl.b], writes=[rl.b])
                P.op("dve", lambda e, bo=bo: e.tensor_tensor(aa[:, :], bo[:, :], rl[:, :], ALU.mult),
                     reads=[bo.b, rl.b], writes=[aa.b])
                P.op("dve", lambda e: e.scalar_tensor_tensor(dd[:, :], aa[:, QG:2 * QG], nlam[:, 0:1], aa[:, 0:QG],
                                                             ALU.mult, ALU.add),
                     reads=[aa.b, nlam.b], writes=[dd.b])
                P.op("act", lambda e: e.activation(dsq[:, :], dd[:, :], AF.Square), reads=[dd.b], writes=[dsq.b])
                self.mm_group(bR[:, 0:QG], [(self.ones[:, :], dsq[:, :])], reads=[dsq.b, self.ones.b], writes=[bR.b])
                P.op("act", lambda e: e.activation(rr[:, :], bR[:, 0:QG], AF.Ln, bias=self.epsc[:, 0:1], scale=1.0 / 128),
                     reads=[bR.b, self.epsc.b], writes=[rr.b])
                P.op("act", lambda e: e.activation(rr[:, :], rr[:, :], AF.Exp, scale=-0.5), reads=[rr.b], writes=[rr.b])
                P.op("dve", lambda e, oo=oo: e.scalar_tensor_tensor(oo[:, :], dd[:, :], gsub[:, 0:1], rr[:, :],
                                                                   ALU.mult, ALU.mult),
                     reads=[dd.b, gsub.b, rr.b], writes=[oo.b])
                self.dma("sp", o_d[hl][:, qs], oo[:, :], [oo.b], [o_d.b], "oo%d" % (grp % 2))
                grp += 1
        P.barrier()
        A.release()
        A.release()

    def phase_A_gla(self, hn_t, wq_d, wkf_d, wkt_d, wv_d, wr_d, wa1_d, wa2b_d, tri_d, gn_d, o_d):
        P, A = self.P, self.A
        HL = 2
        CH = 128
        A.mark()
        wq = self.sb([128, HL, KC, 128], BF16, "gwq")
        wkf = self.sb([128, HL, KC, 128], BF16, "gwkf")
        wkt = self.sb([128, KC, 256], BF16, "gwkt")
        wv = self.sb([128, KC, 512], BF16, "gwv")
        wr = self.sb([128, 4, KC, 128], BF16, "gwr")
        wa1 = self.sb([128, KC, 16], BF16, "gwa1")
        wa2b = self.sb([17, 256], F32, "gwa2b")
        tri = self.sb([128, 128], F32, "gtri")
        gn = self.sb([128, 2], F32, "ggn")
        onec = self.sb([128, 1], F32, "onec")
        for hh in range(HL):
            self.dma("pool", wq[:, hh, :, :], wq_d[hh].rearrange("p (a c) -> p a c", c=128), [wq_d.b], [wq.b], "gwq")
            self.dma("pool", wkf[:, hh, :, :], wkf_d[hh].rearrange("p (a c) -> p a c", c=128), [wkf_d.b], [wkf.b], "gwkf")
        for rc in range(4):
            self.dma("pool", wr[:, rc, :, :], wr_d[rc].rearrange("p (a c) -> p a c", c=128), [wr_d.b], [wr.b], "gwr")
        self.dma("pool", wkt[:, :, :], wkt_d[:, :].rearrange("p (a c) -> p a c", c=256), [wkt_d.b], [wkt.b], "gwkt")
        self.dma("pool", wv[:, :, :], wv_d[:, :].rearrange("p (a c) -> p a c", c=512), [wv_d.b], [wv.b], "gwv")
        self.dma("pool", wa1[:, :, :], wa1_d[:, :].rearrange("p (a c) -> p a c", c=16), [wa1_d.b], [wa1.b], "gwa1")
        self.dma("sp", wa2b[:, :], wa2b_d[:, :], [wa2b_d.b], [wa2b.b], "gwa2b")
        self.dma("sp", tri[:, :], tri_d[:, :], [tri_d.b], [tri.b], "gtri")
        self.dma("sp", gn[:, :], gn_d[:, :], [gn_d.b], [gn.b], "ggn")
        P.op("dve", lambda e: e.memset(onec[:, :], 1.0), writes=[onec.b])

        hnt_s = [self.sb([128, KC, TT], BF16, "ghnt") for _ in range(2)]
        a1aug = self.sb([17, TT], F32, "a1aug")
        qT = self.sb([128, HL, TT], F32, "gqT")
        kT = self.sb([128, HL, TT], F32, "gkT")
        silr = self.sb([128, 4, TT], F32, "silr")
        er = self.sb([128, TT], F32, "ger")
        e1 = self.sb([128, 256], F32, "ge1")
        l_s = [self.sb([128, 256], F32, "gl") for _ in range(2)]
        ent = self.sb([128, 256], F32, "gent")
        ktl_s = [self.sb([128, 256], BF16, "gktl") for _ in range(2)]
        vt_s = [self.sb([128, 512], BF16, "gvt") for _ in range(2)]
        E_s = [self.sb([128, CH], F32, "gE") for _ in range(2)]
        En_s = [self.sb([128, CH], F32, "gEn") for _ in range(2)]
        qtl_s = [self.sb([128, CH], BF16, "gqtl") for _ in range(2)]
        ktT_s = [self.sb([128, CH], BF16, "gktT") for _ in range(2)]
        atm_s = [self.sb([128, CH], BF16, "gatm") for _ in range(2)]
        ua_s = [self.sb([128, 256], F32, "gua") for _ in range(2)]
        St = [self.sb([128, 256], F32, "gS") for _ in range(HL)]
        Sbf = [self.sb([128, 256], BF16, "gSbf") for _ in range(HL)]
        oTt = self.sb([128, 4, TT], F32, "goT")
        sq = self.sb([128, 2, TT], BF16, "gsq")
        rstd = self.sb([128, TT], F32, "grstd")
        yy = self.sb([128, TT], F32, "gyy")
        oo_s = [self.sb([128, 4, TT], BF16, "goo") for _ in range(2)]
        P.op("dve", lambda e: e.memset(a1aug[:, :], 1.0), writes=[a1aug.b])
        for hh in range(HL):
            P.op("dve", lambda e, hh=hh: e.memset(St[hh][:, :], 0.0), writes=[St[hh].b])
        NTL = S // TT
        od_v = o_d.t.ap().rearrange("c p t -> p c t")

        def g_load(i):
            self.dma("sp", hnt_s[i % 2][:, :, :], hn_t[0](i), [hn_t[1]], [hnt_s[i % 2].b],
                     "ghnt%d" % (i % 2))

        g_load(0)
        it = 0
        for i in range(NTL):
            if i + 1 < NTL:
                g_load(i + 1)
            hnt = hnt_s[i % 2]
            bk = self.bank_ring.next()
            self.mm_group(bk[0:16, :], [(wa1[:, kc, :], hnt[:, kc, :]) for kc in range(KC)],
                          reads=[wa1.b, hnt.b], writes=[bk.b])
            P.op("act", lambda e, bk=bk: e.activation(a1aug[0:16, :], bk[0:16, :], AF.Copy),
                 reads=[bk.b], writes=[a1aug.b])
            for hh in range(HL):
                bk = self.bank_ring.next()
                self.mm_group(bk[:, :], [(wq[:, hh, kc, :], hnt[:, kc, :]) for kc in range(KC)],
                              reads=[wq.b, hnt.b], writes=[bk.b])
                P.op("act", lambda e, bk=bk, hh=hh: e.activation(qT[:, hh, :], bk[:, :], AF.Copy, scale=float(128 ** -0.5)),
                     reads=[bk.b], writes=[qT.b])
                bk = self.bank_ring.next()
                self.mm_group(bk[:, :], [(wkf[:, hh, kc, :], hnt[:, kc, :]) for kc in range(KC)],
                              reads=[wkf.b, hnt.b], writes=[bk.b])
                P.op("act", lambda e, bk=bk, hh=hh: e.activation(kT[:, hh, :], bk[:, :], AF.Copy),
                     reads=[bk.b], writes=[kT.b])
            for rc in range(4):
                bk = self.bank_ring.next()
                self.mm_group(bk[:, :], [(wr[:, rc, kc, :], hnt[:, kc, :]) for kc in range(KC)],
                              reads=[wr.b, hnt.b], writes=[bk.b])
                P.op("act", lambda e, bk=bk: e.activation(er[:, :], bk[:, :], AF.Exp, scale=-1.0),
                     reads=[bk.b], writes=[er.b])
                P.op("dve", lambda e: e.tensor_scalar(er[:, :], er[:, :], 1.0, None, ALU.add), reads=[er.b], writes=[er.b])
                P.op("dve", lambda e: e.reciprocal(er[:, :], er[:, :]), reads=[er.b], writes=[er.b])
                P.op("dve", lambda e, bk=bk, rc=rc: e.tensor_tensor(silr[:, rc, :], er[:, :], bk[:, :], ALU.mult),
                     reads=[bk.b, er.b], writes=[silr.b])
            for c4 in range(TT // CH):
                cs = slice(c4 * CH, (c4 + 1) * CH)
                first = (i == 0 and c4 == 0)
                ll = l_s[c4 % 2]
                ktl = ktl_s[c4 % 2]
                vt = vt_s[c4 % 2]
                bz = self.bank_ring.next()
                self.mm_group(bz[:, 0:256], [(a1aug[0:17, cs], wa2b[0:17, :])], reads=[a1aug.b, wa2b.b], writes=[bz.b])
                P.op("act", lambda e, bz=bz: e.activation(e1[:, :], bz[:, 0:256], AF.Exp, scale=-1.0),
                     reads=[bz.b], writes=[e1.b])
                P.op("act", lambda e, ll=ll: e.activation(ll[:, :], e1[:, :], AF.Ln, bias=onec[:, 0:1]),
                     reads=[e1.b, onec.b], writes=[ll.b])
                bb = self.bank_ring.next()
                self.mm_group(bb[:, 0:256], [(tri[:, :], ll[:, :])], reads=[tri.b, ll.b], writes=[bb.b])
                P.op("act", lambda e, bb=bb: e.activation(ent[:, :], bb[:, 0:256], AF.Exp, scale=1.0 / 16),
                     reads=[bb.b], writes=[ent.b])
                bk = self.bank_ring.next()
                self.mm_group(bk[:, 0:256], [(hnt[:, kc, cs], wkt[:, kc, :]) for kc in range(KC)],
                              reads=[wkt.b, hnt.b], writes=[bk.b])
                P.op("dve", lambda e, bk=bk, ktl=ktl: e.tensor_tensor(ktl[:, :], ent[:, :], bk[:, 0:256], ALU.mult),
                     reads=[bk.b, ent.b], writes=[ktl.b])
                bv = self.bank_ring.next()
                self.mm_group(bv[:, :], [(hnt[:, kc, cs], wv[:, kc, :]) for kc in range(KC)],
                              reads=[wv.b, hnt.b], writes=[bv.b])
                P.op("act", lambda e, bv=bv, vt=vt: e.activation(vt[:, :], bv[:, :], AF.Copy), reads=[bv.b], writes=[vt.b])
                for hh in range(HL):
                    E, En = E_s[it % 2], En_s[it % 2]
                    qtl, ktT, atm, ua = qtl_s[it % 2], ktT_s[it % 2], atm_s[it % 2], ua_s[it % 2]
                    it += 1
                    bt = self.bank_ring.next()
                    self.mm_group(bt[:, 0:CH], [(ll[:, hh * 128:(hh + 1) * 128], tri[:, :])], reads=[tri.b, ll.b], writes=[bt.b])
                    P.op("act", lambda e, bt=bt, E=E: e.activation(E[:, :], bt[:, 0:CH], AF.Exp, scale=-1.0 / 16),
                         reads=[bt.b], writes=[E.b])
                    P.op("act", lambda e, bt=bt, En=En: e.activation(En[:, :], bt[:, 0:CH], AF.Exp, scale=1.0 / 16),
                         reads=[bt.b], writes=[En.b])
                    P.op("dve", lambda e, qtl=qtl, E=E, hh=hh, cs=cs: e.tensor_tensor(qtl[:, :], qT[:, hh, cs], E[:, :], ALU.mult),
                         reads=[qT.b, E.b], writes=[qtl.b])
                    P.op("pool", lambda e, ktT=ktT, En=En, hh=hh, cs=cs: e.tensor_tensor(ktT[:, :], kT[:, hh, cs], En[:, :], ALU.mult),
                         reads=[kT.b, En.b], writes=[ktT.b])
                    ba = self.bank_ring.next()
                    self.mm_group(ba[:, 0:CH], [(ktT[:, :], qtl[:, :])], reads=[ktT.b, qtl.b], writes=[ba.b])
                    P.op("dve", lambda e, ba=ba, atm=atm: e.tensor_tensor(atm[:, :], tri[:, :], ba[:, 0:CH], ALU.mult),
                         reads=[ba.b, tri.b], writes=[atm.b])
                    bu = self.bank_ring.next()
                    self.mm_group(bu[:, 0:256], [(ktl[:, hh * 128:(hh + 1) * 128], vt[:, hh * 256:(hh + 1) * 256])],
                                  reads=[ktl.b, vt.b], writes=[bu.b])
                    P.op("act", lambda e, bu=bu, ua=ua, E=E: e.activation(ua[:, :], bu[:, 0:256], AF.Copy, scale=E[:, CH - 1:CH]),
                         reads=[bu.b, E.b], writes=[ua.b])
                    bo = self.bank_ring.next()

                    def omm(e, bo=bo, hh=hh, qtl=qtl, vt=vt, atm=atm, first=first):
                        ins = None
                        for eh in range(2):
                            if not first:
                                e.matmul(bo[:, eh * CH:(eh + 1) * CH], Sbf[hh][:, eh * 128:(eh + 1) * 128], qtl[:, :],
                                         start=True, stop=False)
                            ins = e.matmul(bo[:, eh * CH:(eh + 1) * CH], vt[:, hh * 256 + eh * 128:hh * 256 + (eh + 1) * 128],
                                           atm[:, :], start=first, stop=True)
                        return ins

                    P.op("pe", omm, reads=[Sbf[hh].b, qtl.b, vt.b, atm.b], writes=[bo.b])
                    P.op("act", lambda e, bo=bo, hh=hh, cs=cs: e.activation(
                        oTt[:, 2 * hh:2 * hh + 2, cs], bo[:, 0:2 * CH].rearrange("p (a c) -> p a c", a=2), AF.Copy),
                        reads=[bo.b], writes=[oTt.b])
                    P.op("dve", lambda e, hh=hh, E=E, ua=ua: e.scalar_tensor_tensor(
                        St[hh][:, :], St[hh][:, :], E[:, CH - 1:CH], ua[:, :], ALU.mult, ALU.add),
                        reads=[St[hh].b, E.b, ua.b], writes=[St[hh].b])
                    P.op("pool", lambda e, hh=hh: e.tensor_copy(Sbf[hh][:, :], St[hh][:, :]),
                         reads=[St[hh].b], writes=[Sbf[hh].b])
            oo = oo_s[i % 2]
            for hh in range(HL):
                P.op("act", lambda e, hh=hh: e.activation(sq[:, :, :], oTt[:, 2 * hh:2 * hh + 2, :], AF.Square),
                     reads=[oTt.b], writes=[sq.b])
                bk = self.bank_ring.next()
                self.mm_group(bk[:, :], [(self.ones[:, :], sq[:, 0, :]), (self.ones[:, :], sq[:, 1, :])],
                              reads=[sq.b, self.ones.b], writes=[bk.b])
                P.op("act", lambda e, bk=bk: e.activation(rstd[:, :], bk[:, :], AF.Ln, bias=self.epsc[:, 0:1], scale=1.0 / 256),
                     reads=[bk.b, self.epsc.b], writes=[rstd.b])
                P.op("act", lambda e: e.activation(rstd[:, :], rstd[:, :], AF.Exp, scale=-0.5), reads=[rstd.b], writes=[rstd.b])
                for eh in range(2):
                    P.op("dve", lambda e, hh=hh, eh=eh: e.scalar_tensor_tensor(
                        yy[:, :], oTt[:, 2 * hh + eh, :], gn[:, eh:eh + 1], rstd[:, :], ALU.mult, ALU.mult),
                        reads=[oTt.b, gn.b, rstd.b], writes=[yy.b])
                    P.op("pool", lambda e, hh=hh, eh=eh, oo=oo: e.tensor_tensor(
                        oo[:, 2 * hh + eh, :], yy[:, :], silr[:, 2 * hh + eh, :], ALU.mult),
                        reads=[yy.b, silr.b], writes=[oo.b])
            self.dma("sp", od_v[:, :, i * TT:(i + 1) * TT], oo[:, :, :], [oo.b], [o_d.b], "goo%d" % (i % 2))
        P.barrier()
        A.release()

    def phase_N(self, h, g_d, out_hn):
        P, A = self.P, self.A
        A.mark()
        gN = self.sb([128, KC], F32, "gN")
        self.dma("sp", gN[:, :], g_d[:, :], [g_d.b], [gN.b], "gN")
        htN = [self.sb([128, KC, TT], F32, "htN") for _ in range(2)]
        sqN = self.sb([128, KC, TT], BF16, "sqN")
        rstdN = self.sb([128, TT], F32, "rstdN")
        oN = [self.sb([128, KC, TT], BF16, "oN") for _ in range(2)]
        for ti in range(NTILE):
            sl = slice(ti * TT, (ti + 1) * TT)
            k = ti % 2
            ht, ot = htN[k], oN[k]
            self.dma("sp", ht[:, :, :], h[:, :, sl], [h.b], [ht.b], "htN%d" % k)
            bk = self.bank_ring.next()
            self.rms_stats(ht, TT, sqN, bk, rstdN)
            for kc in range(KC):
                P.op("dve", lambda e, kc=kc, ht=ht, ot=ot: e.scalar_tensor_tensor(
                    ot[:, kc, :], ht[:, kc, :], gN[:, kc:kc + 1], rstdN[:, :], ALU.mult, ALU.mult),
                    reads=[ht.b, gN.b, rstdN.b], writes=[ot.b])
            self.dma("sp", out_hn[:, :, sl], ot[:, :, :], [ot.b], [out_hn.b], "oN%d" % k)
        P.barrier()
        A.release()

    def finish(self):
        self.P.barrier()
        self.P.op("sp", lambda e: e.nop())
        self.P.emit()


SP_GFFN = 0
SP_GNEXT = 8
SP_CONV = 16
NSP = SP_CONV + 44 * 4


def pm(v):
    return np.ascontiguousarray(np.asarray(v, np.float32).reshape(-1, 128).T)


def pack_sp_B(g_ffn, g_next, conv_w, conv_b):
    sp = np.zeros((128, NSP), np.float32)
    sp[:, SP_GFFN:SP_GFFN + KC] = pm(g_ffn)
    sp[:, SP_GNEXT:SP_GNEXT + KC] = pm(g_next)
    cw = np.asarray(conv_w, np.float32).reshape(3, 44, 128)
    cb = np.asarray(conv_b, np.float32).reshape(44, 128)
    blk = np.zeros((128, 44, 4), np.float32)
    blk[:, :, 0:3] = cw.transpose(2, 1, 0)
    blk[:, :, 3] = cb.T
    sp[:, SP_CONV:] = blk.reshape(128, 44 * 4)
    return sp


def lay_wo(w):
    return np.ascontiguousarray(np.asarray(w, np.float32).reshape(KC, 128, D).transpose(1, 0, 2).reshape(128, KC * D))


def lay_wup(w):
    w = np.asarray(w, np.float32).reshape(KC, 128, 2, NJ, 128)
    return np.ascontiguousarray(w.transpose(3, 1, 0, 2, 4).reshape(NJ, 128, KC * 256))


def lay_wdn(w):
    w = np.asarray(w, np.float32).reshape(NJ, 128, KC, 128)
    return np.ascontiguousarray(w.transpose(2, 1, 0, 3).reshape(KC, 128, NJ * 128))


def fm(xT):
    x = np.asarray(xT)
    return np.ascontiguousarray(x.reshape(x.shape[0], KC, 128).transpose(2, 1, 0))


def unfm(a):
    return np.ascontiguousarray(np.asarray(a).transpose(2, 1, 0).reshape(a.shape[2], KC * 128))


def rope_tables():
    inv = (1.0 / (np.float32(10000.0) ** (np.arange(0, 64, 2, dtype=np.float32) / np.float32(64)))).astype(np.float32)
    ang = (np.arange(S, dtype=np.float32)[:, None] * inv[None, :]).astype(np.float32)
    cos, sin = np.cos(ang).astype(np.float32), np.sin(ang).astype(np.float32)
    p = np.arange(128)
    C = cos[:, p % 32].T
    sgn = np.where((p % 64) < 32, -1.0, 1.0).astype(np.float32)
    Sg = sin[:, p % 32].T * sgn[:, None]
    return np.ascontiguousarray(C, np.float32), np.ascontiguousarray(Sg, np.float32)


def diff_consts():
    bf = ml_dtypes.bfloat16
    p = np.arange(128)
    partner = np.where((p % 64) < 32, p + 32, p - 32)
    psw = np.zeros((128, 128), np.float32)
    psw[partner, p] = 1.0
    c = np.arange(256)
    mA = (p[:, None] <= c[None, :]).astype(np.float32)
    mB = (p[:, None] + 128 <= c[None, :]).astype(np.float32)
    cst = np.concatenate([psw, mA, mA, mB, mB], axis=1)
    return cst.astype(bf)


def lay_wqk(w, heads):
    w = np.asarray(w, np.float32).reshape(KC, 128, 8, 128)
    return np.ascontiguousarray(w[:, :, heads, :].transpose(2, 1, 0, 3).reshape(len(heads), 128, KC * 128))


def lay_wv(w, heads):
    w = np.asarray(w, np.float32).reshape(KC, 128, 8, 128)
    return np.ascontiguousarray(w[:, :, heads, :].transpose(1, 0, 2, 3).reshape(128, KC * len(heads) * 128))


def pack_spa_diff(lam_vecs, subln, lam_init):
    lv = np.asarray(lam_vecs, np.float32)
    row = np.concatenate([lv[0], lv[2], lv[1], lv[3]])
    spa = np.zeros((128, 259), np.float32)
    spa[:, 0:256] = row[None, :]
    spa[:, 256] = np.asarray(subln, np.float32)
    spa[:, 257] = -lam_init
    spa[:, 258] = 1.0 - lam_init
    return spa


def lay_fm_chunks(w, cols):
    w = np.asarray(w, np.float32)
    out = np.stack([w[:, c:c + 128].reshape(KC, 128, 128).transpose(1, 0, 2).reshape(128, KC * 128) for c in cols])
    return np.ascontiguousarray(out)


def lay_tm(w):
    w = np.asarray(w, np.float32)
    n = w.shape[1]
    return np.ascontiguousarray(w.reshape(KC, 128, n).transpose(1, 0, 2).reshape(128, KC * n))


def gla_inputs(w_in, w_a1, w_a2, b_a, gnorm, heads):
    w_in = np.asarray(w_in, np.float32)
    qc = [h * 128 for h in heads]
    kc = [512 + h * 128 for h in heads]
    vcols = np.concatenate([np.arange(1024 + h * 256, 1024 + (h + 1) * 256) for h in heads])
    rc = [2048 + h * 256 + eh * 128 for h in heads for eh in range(2)]
    kcols = np.concatenate([np.arange(512 + h * 128, 512 + (h + 1) * 128) for h in heads])
    acols = np.concatenate([np.arange(h * 128, (h + 1) * 128) for h in heads])
    wa2b = np.concatenate([np.asarray(w_a2, np.float32)[:, acols], np.asarray(b_a, np.float32)[None, acols]], 0)
    p = np.arange(128)
    tri = (p[:, None] <= p[None, :]).astype(np.float32)
    return dict(wq=lay_fm_chunks(w_in, qc), wkf=lay_fm_chunks(w_in, kc), wkt=lay_tm(w_in[:, kcols]),
                wv=lay_tm(w_in[:, vcols]), wr=lay_fm_chunks(w_in, rc), wa1=lay_tm(w_a1),
                wa2b=np.ascontiguousarray(wa2b), tri=tri,
                gn=np.ascontiguousarray(np.asarray(gnorm, np.float32).reshape(2, 128).T))


def lam_init_of(layer):
    return float(0.8 - 0.6 * np.exp(-0.3 * layer))


def declare_diff_inputs(B, sfx=""):
    return dict(wq=B.din("dwq" + sfx, [4, 128, KC * 128], F32), wk=B.din("dwk" + sfx, [4, 128, KC * 128], F32),
                wv=B.din("dwv" + sfx, [128, KC * 512], F32), spa=B.din("dspa" + sfx, [128, 259], F32))


def declare_gla_inputs(B, sfx=""):
    return dict(wq=B.din("gwq" + sfx, [2, 128, KC * 128], F32), wkf=B.din("gwkf" + sfx, [2, 128, KC * 128], F32),
                wkt=B.din("gwkt" + sfx, [128, KC * 256], F32), wv=B.din("gwv" + sfx, [128, KC * 512], F32),
                wr=B.din("gwr" + sfx, [4, 128, KC * 128], F32), wa1=B.din("gwa1" + sfx, [128, KC * 16], F32),
                wa2b=B.din("gwa2b" + sfx, [17, 256], F32), gn=B.din("ggn" + sfx, [128, 2], F32))


def declare_B_inputs(B, sfx=""):
    return dict(wo=B.din("wo" + sfx, [128, KC * D], F32), wup=B.din("wup" + sfx, [NJ, 128, KC * 256], F32),
                wdn=B.din("wdn" + sfx, [KC, 128, NJ * 128], F32), sp=B.din("sp" + sfx, [128, NSP], F32))


def build_N():
    nc = bass.Bass("TRN2", target_bir_lowering=False)
    B = Builder(nc)
    x = B.din("x", [128, KC, NT], F32)
    g = B.din("g", [128, KC], F32)
    o = B.dout("hn", [128, KC, NT], BF16)
    B.phase_N(x, g, o)
    B.finish()
    return nc


def build_A_diff():
    nc = bass.Bass("TRN2", target_bir_lowering=False)
    B = Builder(nc)
    hn = B.din("hn", [128, KC, S], BF16)
    w = declare_diff_inputs(B)
    rc = B.din("rc", [128, S], F32)
    rs = B.din("rs", [128, S], F32)
    cst = B.din("cst", [128, 128 + 1024], BF16)
    o = B.dout("o", [4, 128, S], BF16)
    B.phase_A_diff((lambda i: hn[:, :, i * TT:(i + 1) * TT], hn.b), w["wq"], w["wk"], w["wv"], rc, rs, cst, w["spa"], o)
    B.finish()
    return nc


def build_A_gla():
    nc = bass.Bass("TRN2", target_bir_lowering=False)
    B = Builder(nc)
    hn = B.din("hn", [128, KC, S], BF16)
    w = declare_gla_inputs(B)
    tri = B.din("tri", [128, 128], F32)
    o = B.dout("o", [4, 128, S], BF16)
    B.phase_A_gla((lambda i: hn[:, :, i * TT:(i + 1) * TT], hn.b), w["wq"], w["wkf"], w["wkt"], w["wv"], w["wr"],
                  w["wa1"], w["wa2b"], tri, w["gn"], o)
    B.finish()
    return nc


def build_B(final):
    nc = bass.Bass("TRN2", target_bir_lowering=False)
    B = Builder(nc)
    h_in = B.din("h_in", [128, KC, NT], F32)
    oin = B.din("oin", [128, KC, NT], BF16)
    hhalo = B.din("hhalo", [128, KC, 2], F32)
    ohalo = B.din("ohalo", [128, KC, 2], BF16)
    w = declare_B_inputs(B)
    hsc = B.din("hsc", [128, 1], F32)
    h = B.dout("h", [128, KC, NT], F32)
    if final:
        o = B.dout("o", [128, KC, NT], F32)
        B.phase_B(h_in, h, oin, hhalo, ohalo, w["wo"], w["wup"], w["wdn"], w["sp"], hsc, None, o)
    else:
        o = B.dout("o", [128, KC, NT], BF16)
        B.phase_B(h_in, h, oin, hhalo, ohalo, w["wo"], w["wup"], w["wdn"], w["sp"], hsc, o, None)
    B.finish()
    return nc


def host_layer_inputs(inp, layer, r):
    j = layer // 2
    d = {}
    if layer % 2 == 0:
        heads = list(range(4 * r, 4 * r + 4))
        wqkv = np.asarray(inp["diff_w_qkv"][j], np.float32)
        d["A"] = dict(dwq=lay_wqk(wqkv[:, 0:D], heads), dwk=lay_wqk(wqkv[:, D:2 * D], heads),
                      dwv=lay_wv(wqkv[:, 2 * D:], heads),
                      dspa=pack_spa_diff(inp["diff_lambda"][j], inp["diff_subln"][j], lam_init_of(layer)))
        w_o = inp["diff_w_o"][j]
    else:
        g = gla_inputs(inp["gla_w_in"][j], inp["gla_w_a1"][j], inp["gla_w_a2"][j], inp["gla_b_a"][j],
                       inp["gla_norm"][j], [2 * r, 2 * r + 1])
        d["A"] = dict(gwq=g["wq"], gwkf=g["wkf"], gwkt=g["wkt"], gwv=g["wv"], gwr=g["wr"], gwa1=g["wa1"],
                      gwa2b=g["wa2b"], ggn=g["gn"])
        w_o = inp["gla_w_o"][j]
    g_next = inp["norm_mix"][layer + 1] if layer + 1 < DEPTH else inp["norm_final"]
    d["B"] = dict(wo=lay_wo(w_o), wup=lay_wup(inp["ffn_w_up"][layer]), wdn=lay_wdn(inp["ffn_w_down"][layer]),
                  sp=pack_sp_B(inp["norm_ffn"][layer], g_next, inp["ffn_conv_w"][layer], inp["ffn_conv_b"][layer]))
    return d


_PROGS = {}


def _prog(name, fn):
    if name not in _PROGS:
        _PROGS[name] = fn()
    return _PROGS[name]


def kernel_unfused(inp):
    bf = ml_dtypes.bfloat16
    x = np.asarray(inp["x"], np.float32)
    cores = list(range(8))
    C, Sg = rope_tables()
    cst = diff_consts()
    p = np.arange(128)
    tri = (p[:, None] <= p[None, :]).astype(np.float32)
    h = [fm(x[c // 2, (c % 2) * NT:(c % 2 + 1) * NT]) for c in cores]
    g0 = pm(inp["norm_mix"][0])
    res = run_bass_kernel_spmd(_prog("N", build_N), [dict(x=h[c], g=g0) for c in cores], core_ids=cores)
    hn = [np.asarray(res.results[c]["hn"]) for c in cores]
    out = None
    for layer in range(DEPTH):
        li = [host_layer_inputs(inp, layer, r) for r in range(2)]
        hn_full = [np.concatenate([hn[2 * b], hn[2 * b + 1]], axis=2) for b in range(4)]
        maps = []
        for c in cores:
            m = dict(li[c % 2]["A"])
            m["hn"] = hn_full[c // 2]
            if layer % 2 == 0:
                m.update(rc=C, rs=Sg, cst=cst)
            else:
                m.update(tri=tri)
            maps.append(m)
        prog = _prog("Ad", build_A_diff) if layer % 2 == 0 else _prog("Ag", build_A_gla)
        res = run_bass_kernel_spmd(prog, maps, core_ids=cores)
        o = [np.asarray(res.results[c]["o"]) for c in cores]
        maps = []
        for c in cores:
            b, r = c // 2, c % 2
            oall = np.concatenate([o[2 * b], o[2 * b + 1]], axis=0)
            m = dict(li[r]["B"])
            m["h_in"] = h[c]
            m["oin"] = np.ascontiguousarray(oall[:, :, r * NT:(r + 1) * NT].transpose(1, 0, 2))
            if r == 1:
                m["ohalo"] = np.ascontiguousarray(oall[:, :, NT - 2:NT].transpose(1, 0, 2))
                m["hhalo"] = np.ascontiguousarray(h[2 * b][:, :, NT - 2:NT])
            else:
                m["ohalo"] = np.zeros((128, KC, 2), bf)
                m["hhalo"] = np.zeros((128, KC, 2), np.float32)
            m["hsc"] = np.full((128, 1), float(r), np.float32)
            maps.append(m)
        final = layer == DEPTH - 1
        res = run_bass_kernel_spmd(_prog("Bf" if final else "B", lambda: build_B(final)), maps, core_ids=cores)
        h = [np.asarray(res.results[c]["h"]) for c in cores]
        if final:
            out = [np.asarray(res.results[c]["o"]) for c in cores]
        else:
            hn = [np.asarray(res.results[c]["o"]) for c in cores]
    y = np.zeros((4, S, D), np.float32)
    for c in cores:
        y[c // 2, (c % 2) * NT:(c % 2 + 1) * NT] = unfm(out[c])
    return y


def kernel(**inputs):
    return kernel_unfused(inputs)
```
